# Optimizing a Trainium2 kernel written in Bass

```python
import math
import jax
import jax.numpy as jnp
from jax import lax
import numpy as np

D_MODEL = 2048
BATCH = 2
SEQ = 16384
DEPTH = 2

GRID_W = 64
CTX_LEN = 256
EPS = 1e-6
POOL_W = 512
POOL_GROUPS = 4
POOL_GROUP_W = POOL_W // POOL_GROUPS
POOL_WINDOWS = (2, 4, 8, 16)
N_HEADS = 8
Q_LORA = 512
KV_LORA = 256
QK_NOPE = 128
QK_ROPE = 64
QK_HEAD = QK_NOPE + QK_ROPE
V_HEAD = 128
ROPE_FREQS = QK_ROPE // 4
ROPE_THETA = 10000.0
SOFTMAX_SCALE = QK_HEAD ** -0.5
Q_BLOCK = 128
SG_W = 512
SG_GROUPS = 4
SG_GROUP_W = SG_W // SG_GROUPS
CHUNK = 128
N_BRANCH = 3
D_FF = 5632
CONV_W = 3
IN_SPLITS = (POOL_W, POOL_W + Q_LORA, POOL_W + Q_LORA + KV_LORA, POOL_W + Q_LORA + KV_LORA + QK_ROPE, POOL_W + Q_LORA + KV_LORA + QK_ROPE + 2 * SG_W)
IN_COLS = IN_SPLITS[-1] + N_BRANCH * D_MODEL

kernel_name = 'hybrid_pool_mla_sgmlp_convffn_dit'


def rmsnorm(x, g):
    x32 = x.astype(jnp.float32)
    y = x32 * lax.rsqrt(jnp.mean(x32 * x32, axis=-1, keepdims=True) + EPS)
    return (y * g.astype(jnp.float32)).astype(x.dtype)


def layernorm_gain(x, g):
    x32 = x.astype(jnp.float32)
    xc = x32 - jnp.mean(x32, axis=-1, keepdims=True)
    y = xc * lax.rsqrt(jnp.mean(xc * xc, axis=-1, keepdims=True) + EPS)
    return (y * g.astype(jnp.float32)).astype(x.dtype)


def modulate(h, shift, scale):
    return h * (1.0 + scale) + shift


def ada_modulation(s, ada_w, ada_b):
    m = s @ ada_w + ada_b
    return jnp.split(m[:, None, :], 6, axis=-1)


def axial_rope_tables(n_tokens):
    rows = n_tokens // GRID_W
    row = jnp.repeat(jnp.arange(rows, dtype=jnp.float32), GRID_W)
    col = jnp.tile(jnp.arange(GRID_W, dtype=jnp.float32), rows)
    inv_freq = ROPE_THETA ** (-jnp.arange(ROPE_FREQS, dtype=jnp.float32) / ROPE_FREQS)
    ang = jnp.stack([row[:, None] * inv_freq, col[:, None] * inv_freq], axis=1)
    return jnp.cos(ang)[:, None], jnp.sin(ang)[:, None]


def apply_axial_rope(x, cos, sin):
    xr = x.astype(jnp.float32).reshape(x.shape[:-1] + (2, 2, ROPE_FREQS))
    x1, x2 = xr[..., 0, :], xr[..., 1, :]
    out = jnp.stack([x1 * cos - x2 * sin, x2 * cos + x1 * sin], axis=-2)
    return out.reshape(x.shape).astype(x.dtype)


def pool_mix(z, pool_w, pool_scale):
    B, L, _ = z.shape
    cs = jnp.concatenate([jnp.zeros((B, 1, POOL_W), jnp.float32), jnp.cumsum(z.astype(jnp.float32), axis=1)], axis=1)
    t = jnp.arange(L)
    means = []
    for g, w in enumerate(POOL_WINDOWS):
        lo = jnp.clip(t - w // 2, 0, L)
        hi = jnp.clip(t + w // 2, 0, L)
        cg = cs[..., g * POOL_GROUP_W:(g + 1) * POOL_GROUP_W]
        cnt = (hi - lo).astype(jnp.float32)[None, :, None]
        means.append((jnp.take(cg, hi, axis=1) - jnp.take(cg, lo, axis=1)) / cnt)
    pooled = jnp.concatenate(means, axis=-1).astype(z.dtype) - z
    y = jnp.einsum('blgc,gcd->blgd', pooled.reshape(B, L, POOL_GROUPS, POOL_GROUP_W), pool_w)
    return y.reshape(B, L, POOL_W) * pool_scale


def spatial_gating(z, sg_norm_g, sg_w, sg_b):
    B, L, _ = z.shape
    u, v = jnp.split(jax.nn.gelu(z), 2, axis=-1)
    v = layernorm_gain(v, sg_norm_g).reshape(B, L // CHUNK, CHUNK, SG_GROUPS, SG_GROUP_W)
    s = jnp.einsum('gpq,bnqgc->bnpgc', sg_w, v) + sg_b.T[None, None, :, :, None]
    return u * s.reshape(B, L, SG_W)


def mla_queries(z_q, p, rope):
    B, L, _ = z_q.shape
    q = (rmsnorm(z_q, p['q_lat_g']) @ p['w_uq']).reshape(B, L, N_HEADS, QK_HEAD)
    q = rmsnorm(q, p['q_norm_g'])
    if rope is not None:
        q = jnp.concatenate([q[..., :QK_NOPE], apply_axial_rope(q[..., QK_NOPE:], *rope)], axis=-1)
    return q


def mla_keys_values(z_kv, z_kr, p, rope):
    B, L, _ = z_kv.shape
    kv = (rmsnorm(z_kv, p['kv_lat_g']) @ p['w_ukv']).reshape(B, L, N_HEADS, QK_NOPE + V_HEAD)
    k_nope, v = kv[..., :QK_NOPE], kv[..., QK_NOPE:]
    k_rope = jnp.broadcast_to(z_kr[:, :, None, :], (B, L, N_HEADS, QK_ROPE))
    k = rmsnorm(jnp.concatenate([k_nope, k_rope], axis=-1), p['k_norm_g'])
    if rope is not None:
        k = jnp.concatenate([k[..., :QK_NOPE], apply_axial_rope(k[..., QK_NOPE:], *rope)], axis=-1)
    return k, v


def attend_latent(q, k, v):
    B, L, H, Dq = q.shape
    qb = q.reshape(B, L // Q_BLOCK, Q_BLOCK, H, Dq).transpose(1, 0, 2, 3, 4)

    def block(qi):
        s = jnp.einsum('bqhd,bkhd->bhqk', qi, k, preferred_element_type=jnp.float32) * SOFTMAX_SCALE
        pr = jax.nn.softmax(s, axis=-1).astype(v.dtype)
        return jnp.einsum('bhqk,bkhd->bqhd', pr, v)

    o = lax.map(block, qb)
    return o.transpose(1, 0, 2, 3, 4).reshape(B, L, H * V_HEAD)


def attend_context(q, k, v):
    B, C, H, _ = q.shape
    s = jnp.einsum('bqhd,bkhd->bhqk', q, k, preferred_element_type=jnp.float32) * SOFTMAX_SCALE
    pr = jax.nn.softmax(s, axis=-1).astype(v.dtype)
    return jnp.einsum('bhqk,bkhd->bqhd', pr, v).reshape(B, C, H * V_HEAD)


def conv_ffn(h, p):
    L = h.shape[1]
    u = h @ p['ffn_up']
    pad = CONV_W // 2
    up = jnp.pad(u, ((0, 0), (pad, pad), (0, 0)))
    w = p['ffn_conv_w']
    acc = p['ffn_conv_b'] + up[:, 0:L] * w[0]
    for j in range(1, CONV_W):
        acc = acc + up[:, j:j + L] * w[j]
    a, val = jnp.split(acc, 2, axis=-1)
    return (jax.nn.silu(a) * val) @ p['ffn_down']


def merge_branches(z, attn, p):
    z_pool, z_sg, z_gate = z[0], z[4], z[5]
    pool_out = pool_mix(z_pool, p['pool_w'], p['pool_scale'])
    sg_out = spatial_gating(z_sg, p['sg_norm_g'], p['sg_w'], p['sg_b'])
    g_pool, g_mla, g_sg = jnp.split(jax.nn.sigmoid(z_gate), N_BRANCH, axis=-1)
    y = g_pool * (pool_out @ p['w_br_pool']) + g_mla * (attn @ p['w_br_mla']) + g_sg * (sg_out @ p['w_br_sg'])
    return y @ p['w_o']


def hybrid_layer(xl, xc, s_lat, s_ctx, rope, p, ctx_out):
    sh1, sc1, g1, sh2, sc2, g2 = ada_modulation(s_lat, p['ada_w'], p['ada_b'])
    csh1, csc1, cg1, csh2, csc2, cg2 = ada_modulation(s_ctx, p['ada_w'], p['ada_b'])
    hl = modulate(rmsnorm(xl, p['norm1_g']), sh1, sc1)
    hc = modulate(rmsnorm(xc, p['norm1_g']), csh1, csc1)
    zl = jnp.split(hl @ p['w_in'], IN_SPLITS, axis=-1)
    zc = jnp.split(hc @ p['w_in'], IN_SPLITS, axis=-1)
    kc, vc = mla_keys_values(zc[2], zc[3], p, None)
    kl, vl = mla_keys_values(zl[2], zl[3], p, rope)
    ql = mla_queries(zl[1], p, rope)
    attn_l = attend_latent(ql, jnp.concatenate([kl, kc], axis=1), jnp.concatenate([vl, vc], axis=1))
    xl = xl + g1 * merge_branches(zl, attn_l, p)
    xl = xl + g2 * conv_ffn(modulate(rmsnorm(xl, p['norm2_g']), sh2, sc2), p)
    if ctx_out:
        qc = mla_queries(zc[1], p, None)
        attn_c = attend_context(qc, kc, vc)
        xc = xc + cg1 * merge_branches(zc, attn_c, p)
        xc = xc + cg2 * conv_ffn(modulate(rmsnorm(xc, p['norm2_g']), csh2, csc2), p)
    return xl, xc


def setup_inputs(seed: int = 0) -> dict:
    key = jax.random.key(seed)
    ks = jax.random.split(key, 32)
    f32 = jnp.float32

    def nrm(k, shape, scale=1.0):
        return jax.random.normal(k, shape, f32) * scale

    def gain(k, shape, centre=1.0):
        return centre + 0.1 * jax.random.normal(k, shape, f32)

    F2 = 2 * D_FF
    return {
        'x': nrm(ks[0], (BATCH, SEQ, D_MODEL)),
        'c': nrm(ks[1], (BATCH, D_MODEL)),
        'ctx': nrm(ks[2], (BATCH, CTX_LEN, D_MODEL)),
        'c_ctx': nrm(ks[3], (D_MODEL,)),
        'ada_w': nrm(ks[4], (DEPTH, D_MODEL, 6 * D_MODEL), 0.5 * D_MODEL ** -0.5),
        'ada_b': nrm(ks[5], (DEPTH, 6 * D_MODEL), 0.01),
        'norm1_g': gain(ks[6], (DEPTH, D_MODEL)),
        'w_in': nrm(ks[7], (DEPTH, D_MODEL, IN_COLS), D_MODEL ** -0.5),
        'pool_w': nrm(ks[8], (DEPTH, POOL_GROUPS, POOL_GROUP_W, POOL_GROUP_W), POOL_GROUP_W ** -0.5),
        'pool_scale': gain(ks[9], (DEPTH, POOL_W)),
        'q_lat_g': gain(ks[10], (DEPTH, Q_LORA)),
        'w_uq': nrm(ks[11], (DEPTH, Q_LORA, N_HEADS * QK_HEAD), Q_LORA ** -0.5),
        'kv_lat_g': gain(ks[12], (DEPTH, KV_LORA)),
        'w_ukv': nrm(ks[13], (DEPTH, KV_LORA, N_HEADS * (QK_NOPE + V_HEAD)), KV_LORA ** -0.5),
        'q_norm_g': gain(ks[14], (DEPTH, QK_HEAD), 1.5),
        'k_norm_g': gain(ks[15], (DEPTH, QK_HEAD), 1.5),
        'sg_norm_g': gain(ks[16], (DEPTH, SG_W)),
        'sg_w': nrm(ks[17], (DEPTH, SG_GROUPS, CHUNK, CHUNK), CHUNK ** -0.5),
        'sg_b': gain(ks[18], (DEPTH, SG_GROUPS, CHUNK)),
        'w_br_pool': nrm(ks[19], (DEPTH, POOL_W, D_MODEL), POOL_W ** -0.5),
        'w_br_mla': nrm(ks[20], (DEPTH, N_HEADS * V_HEAD, D_MODEL), (N_HEADS * V_HEAD) ** -0.5),
        'w_br_sg': nrm(ks[21], (DEPTH, SG_W, D_MODEL), SG_W ** -0.5),
        'w_o': nrm(ks[22], (DEPTH, D_MODEL, D_MODEL), D_MODEL ** -0.5),
        'norm2_g': gain(ks[23], (DEPTH, D_MODEL)),
        'ffn_up': nrm(ks[24], (DEPTH, D_MODEL, F2), D_MODEL ** -0.5),
        'ffn_conv_w': nrm(ks[25], (DEPTH, CONV_W, F2), 0.3).at[:, CONV_W // 2].add(1.0),
        'ffn_conv_b': nrm(ks[26], (DEPTH, F2), 0.01),
        'ffn_down': nrm(ks[27], (DEPTH, D_FF, D_MODEL), D_FF ** -0.5),
    }


def reference(x, c, ctx, c_ctx, ada_w, ada_b, norm1_g, w_in, pool_w, pool_scale, q_lat_g, w_uq, kv_lat_g, w_ukv, q_norm_g, k_norm_g, sg_norm_g, sg_w, sg_b, w_br_pool, w_br_mla, w_br_sg, w_o, norm2_g, ffn_up, ffn_conv_w, ffn_conv_b, ffn_down):
    s_lat = jax.nn.silu(c)
    s_ctx = jax.nn.silu(c_ctx)[None, :]
    rope = axial_rope_tables(x.shape[1])
    xl, xc = x, ctx
    for i in range(DEPTH):
        p = {
            'ada_w': ada_w[i], 'ada_b': ada_b[i], 'norm1_g': norm1_g[i], 'w_in': w_in[i],
            'pool_w': pool_w[i], 'pool_scale': pool_scale[i],
            'q_lat_g': q_lat_g[i], 'w_uq': w_uq[i], 'kv_lat_g': kv_lat_g[i], 'w_ukv': w_ukv[i],
            'q_norm_g': q_norm_g[i], 'k_norm_g': k_norm_g[i],
            'sg_norm_g': sg_norm_g[i], 'sg_w': sg_w[i], 'sg_b': sg_b[i],
            'w_br_pool': w_br_pool[i], 'w_br_mla': w_br_mla[i], 'w_br_sg': w_br_sg[i], 'w_o': w_o[i],
            'norm2_g': norm2_g[i], 'ffn_up': ffn_up[i], 'ffn_conv_w': ffn_conv_w[i],
            'ffn_conv_b': ffn_conv_b[i], 'ffn_down': ffn_down[i],
        }
        xl, xc = hybrid_layer(xl, xc, s_lat, s_ctx, rope, p, i < DEPTH - 1)
    return xl
```

```python
from contextlib import ExitStack
import numpy as np
import ml_dtypes
import concourse.bass as bass
import concourse.mybir as mybir
from concourse.bass_utils import run_bass_kernel_spmd

F32 = mybir.dt.float32
BF16 = mybir.dt.bfloat16
AF = mybir.ActivationFunctionType
ALU = mybir.AluOpType
AX = mybir.AxisListType

D = 2048
NT = 4096
NCX = 256
L = 16384
DEPTH = 2
NH = 8
INC = 8512
DFF = 5632
EPS = 1e-6
SCALE = 192 ** -0.5
GROUPS = [[0, 1, 2, 3], [4, 5, 6, 7]]


class Buf:
    def __init__(self, t, name, bg=False):
        self.t = t
        self.name = name
        self.lw = None
        self.rd = {}
        self.ds = None
        self.bg = bg
        self.excl = False

    def __getitem__(self, k):
        return self.t[k]


class K:
    def __init__(self, nc, es):
        self.nc = nc
        self.es = es
        self.eng = {'pe': nc.tensor, 'act': nc.scalar, 'dve': nc.vector, 'pool': nc.gpsimd, 'sp': nc.sync}
        self.csem = {e: es.enter_context(nc.semaphore("cs_" + e)) for e in ('pe', 'act', 'dve', 'pool')}
        self.cnt = {e: 0 for e in self.csem}
        self.seen = {e: {} for e in self.eng}
        self.dpool = [[es.enter_context(nc.semaphore("ds%d" % i)), 0, None] for i in range(56)]
        self.bar = es.enter_context(nc.semaphore("bar"))
        self.barv = 0
        self.bufs = []
        self.pes = None
        self.n_ins = 0

    def phase(self):
        self.pes = ExitStack()
        return self.pes

    def sb(self, name, shape, dt):
        self.uid = getattr(self, 'uid', 0) + 1
        name = "%s_u%d" % (name, self.uid)
        t = self.pes.enter_context(self.nc.sbuf_tensor(name, list(shape), dt))
        b = Buf(t, name)
        self.bufs.append(b)
        return b

    def dr(self, name, shape, dt, kind="Internal", bg=False):
        t = self.nc.dram_tensor(name, list(shape), dt, kind=kind)
        b = Buf(t, name, bg=bg)
        return b

    def _wait(self, e, tok):
        if tok is None:
            return
        sem, val, te = tok
        if te == 'pe' and e == 'pe':
            return
        if te in self.cnt and val > self.cnt[te]:
            raise RuntimeError("wait on unissued milestone %s %d>%d" % (te, val, self.cnt[te]))
        sid = id(sem)
        if self.seen[e].get(sid, 0) >= val:
            return
        self.seen[e][sid] = val
        self.eng[e].wait_ge(sem, val)
        self.n_ins += 1

    def _deps(self, e, r, w):
        for b in r:
            self._wait(e, b.lw)
            if b.excl:
                for kk, tok in b.rd.items():
                    if tok[2] != e:
                        self._wait(e, tok)
        for b in w:
            self._wait(e, b.lw if (b.lw is None or b.lw[2] != e or e == 'sp') else None)
            for k, tok in b.rd.items():
                if tok[2] == e and e != 'sp' and e != 'poolq':
                    continue
                self._wait(e, tok)

    def C(self, e, r, w, fn, inc=True):
        self._deps(e, r, w)
        ins = fn(self.eng[e])
        self.n_ins += 1
        if inc:
            self.cnt[e] += 1
            ins.then_inc(self.csem[e], 1)
            tok = (self.csem[e], self.cnt[e], e)
        else:
            tok = (self.csem[e], self.cnt[e] + 1, e)
        for b in w:
            b.lw = tok
            b.rd = {}
        for b in r:
            b.rd[e] = tok
        return ins

    def Dm(self, q, out, in_, r, w, sb):
        self._deps(q, r, w)
        if sb.ds is None:
            for i, ent in enumerate(self.dpool):
                if ent[2] is None:
                    ent[2] = sb
                    sb.ds = i
                    break
            else:
                raise RuntimeError("out of dma sems")
        ent = self.dpool[sb.ds]
        ins = self.eng[q].dma_start(out=out, in_=in_)
        ent[1] += 1
        ins.then_inc(ent[0], 16)
        self.n_ins += 1
        tok = (ent[0], 16 * ent[1], 'dma')
        for b in w:
            b.lw = tok
            b.rd = {}
        for b in r:
            b.rd['dma%d' % sb.ds] = tok
        return ins

    def barrier(self, extra=None):
        sp = self.eng['sp']
        for e in self.cnt:
            self._wait('sp', (self.csem[e], self.cnt[e], e))
        for ent in self.dpool:
            if ent[2] is not None and not ent[2].bg and ent[1] > 0:
                self._wait('sp', (ent[0], 16 * ent[1], 'dma'))
        if extra is not None:
            extra()
        self.barv += 1
        sp.sem_inc(self.bar, 1)
        for e in ('pe', 'act', 'dve', 'pool'):
            self.eng[e].wait_ge(self.bar, self.barv)
        for ent in self.dpool:
            if ent[2] is not None and not ent[2].bg:
                ent[2].ds = None
                ent[2] = None
        for b in self.bufs:
            b.lw = None
            b.rd = {}
        self.bufs = [b for b in self.bufs if b.bg]

    def end_phase(self):
        self.barrier()
        self.pes.close()
        self.pes = None


def build_program():
    nc = bass.Bass("TRN2", target_bir_lowering=False)
    es = ExitStack()
    k = K(nc, es)

    def ext_in(name, shape, dt=F32):
        return k.dr(name, shape, dt, kind="ExternalInput")

    x_in = ext_in("x", [NT, D])
    xh_in = ext_in("xh", [16, D])
    ctx_in = ext_in("ctx", [NCX, D])
    cvT = ext_in("cvT", [128, 6])
    W = {}
    wshapes = {"w_in": [D, INC], "w_uq": [512, 1536], "w_ukv": [256, 2048], "w_br_pool": [512, D],
               "w_br_mla": [1024, D], "w_br_sg": [512, D], "w_o": [D, D], "ffn_up": [D, 2 * DFF],
               "ffn_down": [DFF, D], "pool_w": [512, 128], "sg_w": [512, 128]}
    for n, s in wshapes.items():
        W[n] = ext_in(n, [DEPTH, s[0] // 8, s[1]])
    ada_w = ext_in("ada_w", [DEPTH, D // 8, 6 * D])
    selb = ext_in("selb", [1, 2])
    adabT = ext_in("adabT", [DEPTH, 128, 96])
    n1T = ext_in("n1T", [DEPTH, 128, 16])
    n2T = ext_in("n2T", [DEPTH, 128, 16])
    pscT = ext_in("pscT", [DEPTH, 128, 4])
    cwT = ext_in("cwT", [DEPTH, 128, 3 * 88])
    cbT = ext_in("cbT", [DEPTH, 128, 88])
    qlg = ext_in("qlg", [DEPTH, 512])
    kvlg = ext_in("kvlg", [DEPTH, 256])
    qng = ext_in("qng", [DEPTH, 192])
    kng = ext_in("kng", [DEPTH, 192])
    sgng = ext_in("sgng", [DEPTH, 512])
    sgb = ext_in("sgb", [DEPTH, 512])
    identf = ext_in("identf", [128, 128])
    rope_q = ext_in("rope_q", [NT, 512])
    rope_k = ext_in("rope_k", [L, 64])
    invc = ext_in("invc", [4, NT + NCX])
    m_b2 = ext_in("m_b2", [9, 16])
    m_c = ext_in("m_c", [10, 2])
    sel2 = ext_in("sel2", [8, 2])
    sel3 = ext_in("sel3", [64, 16])
    out_d = k.dr("out", [NT, D], F32, kind="ExternalOutput")

    Wb = {}
    for n, s in wshapes.items():
        Wb[n] = [k.dr("%s_b%d" % (n, l), s, BF16, bg=True) for l in range(DEPTH)]
    Wst = {}
    for n, s in wshapes.items():
        Wst[n] = [k.dr("%s_s%d" % (n, l), [s[0] // 8, s[1]], BF16) for l in range(DEPTH)]
    mp_in = k.dr("mp_in", [128, DEPTH * 288], F32)
    mp_out = k.dr("mp_out", [8 * 128, DEPTH * 288], F32)
    x0e = k.dr("x0e", [NT + 16, D], F32)
    c0e = k.dr("c0e", [NCX + 16, D], F32)
    xmid = k.dr("xmid", [NT + 2, D], F32)
    cmid = k.dr("cmid", [NCX + 2, D], F32)
    x1e = k.dr("x1e", [NT + 16, D], F32)
    c1e = k.dr("c1e", [NCX + 16, D], F32)
    NLC = max(1, min(8, NT // 512))
    LCH = NT // NLC
    lat_loc = [k.dr("lat_loc%d" % i, [LCH, 320], F32) for i in range(NLC)]
    lat_all = [k.dr("lat_all%d" % i, [4 * LCH, 320], F32) for i in range(NLC)]
    lat_ctx = k.dr("lat_ctx", [NCX, 320], F32)
    NQ = NT + NCX
    NK = L + NCX
    NKT = NK // 128
    qT_d = k.dr("qT_d", [NH, 192, NQ], BF16)
    kT_d = k.dr("kT_d", [NH, 192, NK], BF16)
    vv_d = k.dr("vv_d", [NH, 128, NKT, 129], BF16)
    attnT_d = k.dr("attnT_d", [1024, NQ], BF16)
    ex2_in = k.dr("ex2_in", [2, D], F32)
    ex2_out = k.dr("ex2_out", [8, D], F32)
    ex3_in = k.dr("ex3_in", [16, D], F32)
    ex3_out = k.dr("ex3_out", [64, D], F32)

    cc_sem = es.enter_context(nc.semaphore("cc_sem"))
    cc_n = [0]

    k.pes = es
    ident = k.sb("ident", [128, 128], BF16)
    identF = k.sb("identF", [128, 128], F32)
    mT = k.sb("mT", [128, DEPTH, 96, 2], F32)
    epsT = k.sb("epsT", [128, 1], F32)
    ones1 = k.sb("ones1", [1, 128], BF16)
    zrow = k.sb("zrow", [16, D], F32)
    for b in (ident, identF, mT, epsT, ones1, zrow):
        b.bg = True
    banks = []
    for i in range(8):
        t = es.enter_context(nc.psum_tensor("bank%d" % i, [128, 512], F32))
        b = Buf(t, "bank%d" % i, bg=True)
        b.bf = t.bitcast(BF16)
        b.excl = True
        banks.append(b)
        k.bufs.append(b)
    k.pes = None

    def all_gather(src, dst, groups=GROUPS):
        def fn(g):
            return g.collective_compute("AllGather", ALU.bypass, replica_groups=groups,
                                        ins=[src.t.ap().opt()], outs=[dst.t.ap().opt()])
        k._deps('pool', [src], [dst])
        ins = fn(nc.gpsimd)
        cc_n[0] += 1
        ins.then_inc(cc_sem, 1)
        tok = (cc_sem, cc_n[0], 'cc')
        dst.lw = tok
        dst.rd = {}
        src.rd['cc'] = tok
        k._wait('pool', tok)

    k.phase()
    k.Dm('sp', identF[:], identf[:, :], [identf], [identF], identF)
    for i_ in range(8):
        r0_ = i_ * (NT // 8)
        k.Dm('sp', x0e[8 + r0_:8 + r0_ + NT // 8, :], x_in[r0_:r0_ + NT // 8, :], [x_in], [x0e], x0e)
    k.Dm('sp', x0e[0:8, :], xh_in[0:8, :], [xh_in], [x0e], x0e)
    k.Dm('sp', x0e[8 + NT:16 + NT, :], xh_in[8:16, :], [xh_in], [x0e], x0e)
    k.Dm('sp', c0e[8:8 + NCX, :], ctx_in[:, :], [ctx_in], [c0e], c0e)
    k.C('dve', [], [zrow], lambda e: e.memset(zrow[:], 0.0))
    k.C('dve', [], [epsT], lambda e: e.memset(epsT[:], EPS))
    k.C('dve', [], [ones1], lambda e: e.memset(ones1[:], 1.0))
    k.C('dve', [identF], [ident], lambda e: e.tensor_copy(out=ident[:], in_=identF[:]))
    for dst in (c0e, c1e):
        k.Dm('sp', dst[0:8, :], zrow[0:8, :], [zrow], [dst], zrow)
        k.Dm('sp', dst[8 + NCX:16 + NCX, :], zrow[0:8, :], [zrow], [dst], zrow)
    k.Dm('sp', cmid[0:1, :], zrow[0:1, :], [zrow], [cmid], zrow)
    k.Dm('sp', cmid[NCX + 1:NCX + 2, :], zrow[0:1, :], [zrow], [cmid], zrow)
    ALL8 = [list(range(8))]
    wstage_sem = [Buf(None, "wc%d" % l, bg=True) for l in range(DEPTH)]

    def gather_weights(l):
        order = ["w_in", "w_uq", "w_ukv", "pool_w", "sg_w", "w_br_pool", "w_br_mla", "w_br_sg", "w_o", "ffn_up", "ffn_down"]
        for n in order:
            k.Dm('pool', Wst[n][l][:, :], W[n][l, :, :], [W[n]], [Wst[n][l]], wstage_sem[l])
        ent = k.dpool[wstage_sem[l].ds]
        tok = (ent[0], 16 * ent[1], 'dma')
        for n in order:
            Wst[n][l].lw = tok
        for n in order:
            all_gather(Wst[n][l], Wb[n][l], groups=ALL8)

    sT = k.sb("sT", [128, 6], F32)
    cv = k.sb("cv", [128, 6], F32)
    abT = k.sb("abT", [128, DEPTH, 96], F32)
    sb_ = k.sb("selb_bc", [128, 2], F32)
    k.Dm('sp', cv[:], cvT[:, :], [cvT], [cv], cv)
    k.Dm('sp', sb_[:], selb[0:1, :].to_broadcast([128, 2]), [selb], [sb_], sb_)
    for l in range(DEPTH):
        k.Dm('sp', abT[:, l, :], adabT[l, :, :], [adabT], [abT], abT)
    k.C('act', [cv], [sT], lambda e: e.activation(out=sT[:], in_=cv[:], func=AF.Silu))
    aw = [k.sb("aw%d" % i, [128, 2, 512], F32) for i in range(2)]
    mpart = k.sb("mpart", [128, DEPTH * 288], F32)
    for l in range(DEPTH):
        awv = ada_w.t.ap()[l].rearrange("(kt p) c -> p kt c", p=128)
        for ch in range(24):
            a = aw[ch % 2]
            k.Dm('sp', a[:], awv[:, :, ch * 512:(ch + 1) * 512], [ada_w], [a], a)
            for jb in range(4):
                blk = ch * 4 + jb
                for kt in range(2):
                    k.C('pe', [a, sT], [banks[l]],
                        lambda e, kt=kt, jb=jb, blk=blk, a=a, l=l: e.matmul(
                            banks[l][:, blk * 3:blk * 3 + 3], lhsT=a[:, kt, jb * 128:(jb + 1) * 128],
                            rhs=sT[:, kt * 3:kt * 3 + 3], start=(kt == 0), stop=(kt == 1)),
                        inc=(kt == 1))
        k.C('act', [banks[l]], [mpart], lambda e, l=l: e.copy(out=mpart[:, l * 288:(l + 1) * 288], in_=banks[l][:, 0:288]))
    k.Dm('sp', mp_in[:, :], mpart[:], [mpart], [mp_in], mpart)
    k.barrier()
    all_gather(mp_in, mp_out, groups=ALL8)
    gather_weights(0)
    mall = k.sb("mall", [128, 8, DEPTH * 288], F32)
    k.Dm('sp', mall[:], mp_out.t.ap().rearrange("(r p) c -> p r c", p=128), [mp_out], [mall], mall)
    for r in range(1, 8):
        k.C('dve', [mall], [mall], lambda e, r=r: e.tensor_tensor(out=mall[:, 0, :], in0=mall[:, 0, :], in1=mall[:, r, :], op=ALU.add))
    for l in range(DEPTH):
        v3 = mall[:, 0, l * 288:(l + 1) * 288].rearrange("p (b r) -> p b r", r=3)
        k.C('dve', [mall, sb_], [mpart], lambda e, v3=v3: e.tensor_scalar(
            out=mpart[:, 0:96], in0=v3[:, :, 0], scalar1=sb_[:, 0:1], scalar2=None, op0=ALU.mult))
        k.C('dve', [mall, sb_, mpart], [mpart], lambda e, v3=v3: e.scalar_tensor_tensor(
            out=mpart[:, 0:96], in0=v3[:, :, 1], scalar=sb_[:, 1:2], in1=mpart[:, 0:96], op0=ALU.mult, op1=ALU.add))
        k.C('dve', [mpart, abT], [mT], lambda e, l=l: e.tensor_tensor(
            out=mT[:, l, :, 0], in0=mpart[:, 0:96], in1=abT[:, l, :], op=ALU.add))
        k.C('dve', [mall, abT], [mT], lambda e, l=l, v3=v3: e.tensor_tensor(
            out=mT[:, l, :, 1], in0=v3[:, :, 2], in1=abT[:, l, :], op=ALU.add))
    k.end_phase()

    def load_vecT(name, src_ap, n):
        b = k.sb(name, [128, n], F32)
        k.Dm('sp', b[:], src_ap, [], [b], b)
        return b

    def load_bc(name, row_ap, n, dt=F32):
        b = k.sb(name, [128, n], dt)
        k.Dm('sp', b[:], row_ap.to_broadcast([128, n]), [], [b], b)
        return b

    def mod_vecs(l, which, n_gT, r, tag):
        base = 0 if which == 0 else 48
        ge = k.sb("ge" + tag, [128, 16], F32)
        sh = k.sb("sh" + tag, [128, 16], F32)
        k.C('dve', [mT, n_gT], [ge], lambda e: e.scalar_tensor_tensor(
            out=ge[:], in0=mT[:, l, base + 16:base + 32, r], scalar=1.0, in1=n_gT[:],
            op0=ALU.add, op1=ALU.mult))
        k.C('dve', [mT], [sh], lambda e: e.tensor_copy(out=sh[:], in_=mT[:, l, base:base + 16, r]))
        return ge, sh

    def make_bc(l, blk0, r, bc, tmp):
        for kt in range(16):
            k.C('dve', [mT], [tmp], lambda e, kt=kt: e.tensor_copy(
                out=tmp[:], in_=mT[:, l, blk0 + kt, r:r + 1].to_broadcast([128, 128])))
            bk = banks[kt % 2]
            k.C('pe', [tmp, identF], [bk], lambda e, bk=bk: e.matmul(
                bk[:, 0:128], lhsT=tmp[:], rhs=identF[:], start=True, stop=True))
            k.C('act', [bk], [bc], lambda e, kt=kt, bk=bk: e.copy(out=bc[:, kt * 128:(kt + 1) * 128], in_=bk[:, 0:128]))
        return bc

    class NormBufs:
        pass

    def norm_setup(tag, nbuf=2):
        nb = NormBufs()
        nb.xt = [k.sb("xt%s%d" % (tag, i), [128, D], F32) for i in range(nbuf)]
        nb.xs = [k.sb("xs%s%d" % (tag, i), [128, D], BF16) for i in range(nbuf)]
        nb.st = [k.sb("st%s%d" % (tag, i), [128, 4], F32) for i in range(nbuf)]
        nb.n = nbuf
        nb.i = 0
        return nb

    def norm_tile(nb, src, src_ap, n, ge, sh, hT, c0, pb, mask=None):
        i = nb.i % nb.n
        nb.i += 1
        xt, xs, st = nb.xt[i], nb.xs[i], nb.st[i]
        k.Dm('sp', xt[0:n, :], src_ap, [src], [xt], xt)
        k.C('act', [xt], [xs, st], lambda e: e.activation(out=xs[0:n, :], in_=xt[0:n, :], func=AF.Square,
                                                          accum_out=st[0:n, 0:1]))
        k.C('act', [st, epsT], [st], lambda e: e.activation(out=st[0:n, 1:2], in_=st[0:n, 0:1], func=AF.Sqrt,
                                                            scale=1.0 / D, bias=epsT[0:n, 0:1]))
        k.C('dve', [st], [st], lambda e: e.reciprocal(out=st[0:n, 2:3], in_=st[0:n, 1:2]))
        k.C('act', [xt, st], [xs], lambda e: e.activation(out=xs[0:n, :], in_=xt[0:n, :], func=AF.Identity,
                                                          scale=st[0:n, 2:3]))
        for half in range(2):
            bk = pb[half]
            for j in range(8):
                kt = half * 8 + j
                k.C('pe', [xs, ident], [bk], lambda e, kt=kt, j=j, bk=bk: e.transpose(
                    bk.bf[:, j * 128:j * 128 + n], xs[0:n, kt * 128:(kt + 1) * 128], ident[0:n, 0:n]),
                    inc=(j == 7))
            for j in range(8):
                kt = half * 8 + j
                if half == 0:
                    k.C('act', [bk, ge, sh], [hT], lambda e, kt=kt, j=j, bk=bk: e.activation(
                        out=hT[:, kt, c0:c0 + n], in_=bk.bf[:, j * 128:j * 128 + n], func=AF.Identity,
                        scale=ge[:, kt:kt + 1], bias=sh[:, kt:kt + 1]))
                else:
                    k.C('dve', [bk, ge, sh], [hT], lambda e, kt=kt, j=j, bk=bk: e.tensor_scalar(
                        out=hT[:, kt, c0:c0 + n], in0=bk.bf[:, j * 128:j * 128 + n],
                        scalar1=ge[:, kt:kt + 1], scalar2=sh[:, kt:kt + 1], op0=ALU.mult, op1=ALU.add))

    def rstd_from_ssq(st_buf, n, cin, cout, nfeat, ncols=1):
        k.C('act', [st_buf, epsT], [st_buf], lambda e: e.activation(
            out=st_buf[0:n, cout:cout + ncols], in_=st_buf[0:n, cin:cin + ncols], func=AF.Sqrt,
            scale=1.0 / nfeat, bias=epsT[0:n, 0:1]))
        k.C('dve', [st_buf], [st_buf], lambda e: e.reciprocal(
            out=st_buf[0:n, cout:cout + ncols], in_=st_buf[0:n, cout:cout + ncols]))

    xin_e = [x0e, x1e]
    cin_e = [c0e, c1e]
    xout_e = [x1e, None]
    cout_e = [c1e, None]

    for l in range(DEPTH):
        ctx_out = (l < DEPTH - 1)
        XE, CE = xin_e[l], cin_e[l]
        k.phase()
        n1 = load_vecT("n1", n1T[l, :, :], 16)
        geL, shL = mod_vecs(l, 0, n1, 0, "L")
        geC, shC = mod_vecs(l, 0, n1, 1, "C")
        wA = k.sb("wA", [128, 16, 832], BF16)
        k.Dm('sp', wA[:], Wb["w_in"][l].t.ap().rearrange("(kt p) c -> p kt c", p=128)[:, :, 512:1344],
             [Wb["w_in"][l]], [wA], wA)
        wq = k.sb("wq", [128, 4, 1536], BF16)
        k.Dm('sp', wq[:], Wb["w_uq"][l].t.ap().rearrange("(kt p) c -> p kt c", p=128), [Wb["w_uq"][l]], [wq], wq)
        qlg_bc = load_bc("qlg_bc", qlg[l:l + 1, :], 512)
        qng_bc = load_bc("qng_bc", qng[l:l + 1, :], 192)
        nb = norm_setup("A")
        hTa = [k.sb("hTa%d" % i, [128, 16, 128], BF16) for i in range(2)]
        latsb = [k.sb("latsb%d" % i, [128, 320], F32) for i in range(2)]
        stq = [k.sb("stq%d" % i, [128, 32], F32) for i in range(2)]
        zqn = [k.sb("zqn%d" % i, [128, 512], BF16) for i in range(2)]
        zqnT = [k.sb("zqnT%d" % i, [128, 4, 128], BF16) for i in range(2)]
        q_sb = [k.sb("q_sb%d" % i, [128, 8, 12, 16], F32) for i in range(2)]
        sqj = k.sb("sqj", [128, 1536], F32)
        rt = [k.sb("rt%d" % i, [128, 512], F32) for i in range(2)]
        rtmp = [k.sb("rtmp%d" % i, [128, 8, 2, 16], F32) for i in range(4)]
        qr = [k.sb("qr%d" % i, [128, 8, 192], BF16) for i in range(2)]
        qTn = [k.sb("qTn%d" % i, [128, 8, 128], BF16) for i in range(2)]
        qTr = [k.sb("qTr%d" % i, [64, 8, 128], BF16) for i in range(2)]
        tiles = [(XE, 8 + i * 128, i * 128, False) for i in range(NT // 128)]
        tiles += [(CE, 8 + i * 128, NT + i * 128, True) for i in range(NCX // 128)]
        for ti, (src, r0, tok0, isc) in enumerate(tiles):
            i = ti % 2
            hT = hTa[i]
            norm_tile(nb, src, src[r0:r0 + 128, :], 128, geC if isc else geL, shC if isc else shL, hT, 0,
                      (banks[0], banks[1]))
            for kt in range(16):
                k.C('pe', [hT, wA], [banks[2]], lambda e, kt=kt: e.matmul(
                    banks[2][:, :], lhsT=hT[:, kt, :], rhs=wA[:, kt, 0:512], start=(kt == 0), stop=(kt == 15)),
                    inc=(kt == 15))
            for kt in range(16):
                k.C('pe', [hT, wA], [banks[3]], lambda e, kt=kt: e.matmul(
                    banks[3][:, 0:320], lhsT=hT[:, kt, :], rhs=wA[:, kt, 512:832], start=(kt == 0), stop=(kt == 15)),
                    inc=(kt == 15))
            ls = latsb[i]
            k.C('act', [banks[3]], [ls], lambda e: e.copy(out=ls[:], in_=banks[3][:, 0:320]))
            if isc:
                k.Dm('sp', lat_ctx[tok0 - NT:tok0 - NT + 128, :], ls[:], [ls], [lat_ctx], ls)
            else:
                lc = lat_loc[tok0 // LCH]
                k.Dm('sp', lc[tok0 % LCH:tok0 % LCH + 128, :], ls[:], [ls], [lc], ls)
            if isc and not ctx_out:
                continue
            st = stq[i]
            k.C('act', [banks[2]], [sqj, st], lambda e: e.activation(
                out=sqj[:, 0:512], in_=banks[2][:, :], func=AF.Square, accum_out=st[:, 0:1]))
            rstd_from_ssq(st, 128, 0, 1, 512)
            zq = zqn[i]
            k.C('dve', [banks[2], st, qlg_bc], [zq], lambda e: e.scalar_tensor_tensor(
                out=zq[:], in0=banks[2][:, :], scalar=st[:, 1:2], in1=qlg_bc[:], op0=ALU.mult, op1=ALU.mult))
            for j in range(4):
                k.C('pe', [zq, ident], [banks[4]], lambda e, j=j: e.transpose(
                    banks[4].bf[:, j * 128:(j + 1) * 128], zq[:, j * 128:(j + 1) * 128], ident[:]), inc=(j == 3))
            zT = zqnT[i]
            k.C('act', [banks[4]], [zT], lambda e: e.copy(
                out=zT[:], in_=banks[4].bf[:, 0:512].rearrange("p (a b) -> p a b", b=128)))
            qs = q_sb[i]
            qsf = qs[:].rearrange("p h a b -> p (h a b)")
            for n3 in range(3):
                bk = banks[5 + n3]
                for kt in range(4):
                    k.C('pe', [zT, wq], [bk], lambda e, kt=kt, bk=bk, n3=n3: e.matmul(
                        bk[:, :], lhsT=zT[:, kt, :], rhs=wq[:, kt, n3 * 512:(n3 + 1) * 512],
                        start=(kt == 0), stop=(kt == 3)), inc=(kt == 3))
                k.C('act', [bk], [qs], lambda e, bk=bk, n3=n3: e.copy(out=qsf[:, n3 * 512:(n3 + 1) * 512], in_=bk[:, :]))
            k.C('dve', [qs], [sqj], lambda e: e.tensor_tensor(out=sqj[:], in0=qsf, in1=qsf, op=ALU.mult))
            k.C('dve', [sqj], [st], lambda e: e.tensor_reduce(
                out=st[:, 8:16], in_=sqj[:].rearrange("p (h d) -> p h d", d=192), axis=AX.X, op=ALU.add))
            rstd_from_ssq(st, 128, 8, 16, 192, ncols=8)
            for h in range(NH):
                k.C('dve', [qs, st, qng_bc], [qs], lambda e, h=h: e.scalar_tensor_tensor(
                    out=qs[:, h, :, :].rearrange("p a b -> p (a b)"), in0=qs[:, h, :, :].rearrange("p a b -> p (a b)"),
                    scalar=st[:, 16 + h:17 + h], in1=qng_bc[:], op0=ALU.mult, op1=ALU.mult))
            qo = qr[i]
            qo4 = qo[:].rearrange("p h (a b) -> p h a b", b=16)
            k.C('act', [qs], [qo], lambda e: e.copy(out=qo4[:, :, 0:8, :], in_=qs[:, :, 0:8, :]))
            if not isc:
                rtb = rt[i]
                k.Dm('sp', rtb[:], rope_q[tok0:tok0 + 128, :], [rope_q], [rtb], rtb)
                cosv = rtb[:, 0:256].rearrange("p (h a f) -> p h a f", a=2, f=16)
                sinv = rtb[:, 256:512].rearrange("p (h a f) -> p h a f", a=2, f=16)
                x0v = qs[:, :, 8:12:2, :]
                x1v = qs[:, :, 9:12:2, :]
                A_, B_, C_, D_ = rtmp
                k.C('dve', [qs, rtb], [A_], lambda e: e.tensor_tensor(out=A_[:], in0=x0v, in1=cosv, op=ALU.mult))
                k.C('pool', [qs, rtb], [B_], lambda e: e.tensor_tensor(out=B_[:], in0=x1v, in1=sinv, op=ALU.mult))
                k.C('dve', [qs, rtb], [C_], lambda e: e.tensor_tensor(out=C_[:], in0=x1v, in1=cosv, op=ALU.mult))
                k.C('pool', [qs, rtb], [D_], lambda e: e.tensor_tensor(out=D_[:], in0=x0v, in1=sinv, op=ALU.mult))
                k.C('dve', [A_, B_], [qo], lambda e: e.tensor_tensor(out=qo4[:, :, 8:12:2, :], in0=A_[:], in1=B_[:], op=ALU.subtract))
                k.C('dve', [C_, D_], [qo], lambda e: e.tensor_tensor(out=qo4[:, :, 9:12:2, :], in0=C_[:], in1=D_[:], op=ALU.add))
            else:
                k.C('act', [qs], [qo], lambda e: e.copy(out=qo4[:, :, 8:12, :], in_=qs[:, :, 8:12, :]))
            for h in range(NH):
                k.C('pe', [qo, ident], [banks[0]], lambda e, h=h: e.transpose(
                    banks[0].bf[:, h * 128:(h + 1) * 128], qo[:, h, 0:128], ident[:]), inc=(h == 7))
            for h in range(NH):
                k.C('pe', [qo, ident], [banks[1]], lambda e, h=h: e.transpose(
                    banks[1].bf[0:64, h * 128:(h + 1) * 128], qo[:, h, 128:192], ident[:]), inc=(h == 7))
            tn, tr = qTn[i], qTr[i]
            k.C('act', [banks[0]], [tn], lambda e: e.copy(
                out=tn[:], in_=banks[0].bf[:, 0:1024].rearrange("p (h t) -> p h t", t=128)))
            k.C('dve', [banks[1]], [tr], lambda e: e.tensor_copy(
                out=tr[:], in_=banks[1].bf[0:64, 0:1024].rearrange("p (h t) -> p h t", t=128)))
            k.Dm('sp', qT_d.t.ap()[:, 0:128, tok0:tok0 + 128].rearrange("h d t -> d h t"), tn[:], [tn], [qT_d], tn)
            k.Dm('sp', qT_d.t.ap()[:, 128:192, tok0:tok0 + 128].rearrange("h d t -> d h t"), tr[:], [tr], [qT_d], tr)
        k.barrier()
        for ci in range(NLC):
            all_gather(lat_loc[ci], lat_all[ci])
        k.end_phase()

        k.phase()
        wkv = k.sb("wkv", [128, 2, 2048], BF16)
        k.Dm('sp', wkv[:], Wb["w_ukv"][l].t.ap().rearrange("(kt p) c -> p kt c", p=128), [Wb["w_ukv"][l]], [wkv], wkv)
        kvlg_bc = load_bc("kvlg_bc", kvlg[l:l + 1, :], 256)
        kng_bc = load_bc("kng_bc", kng[l:l + 1, :], 192)
        lt = [k.sb("lt%d" % i, [128, 320], F32) for i in range(3)]
        rk = [k.sb("rk%d" % i, [128, 64], F32) for i in range(3)]
        stk = [k.sb("stk%d" % i, [128, 32], F32) for i in range(2)]
        junk = k.sb("junk", [128, 256], F32)
        zn = [k.sb("zn%d" % i, [128, 256], BF16) for i in range(2)]
        znT = [k.sb("znT%d" % i, [128, 2, 128], BF16) for i in range(2)]
        kn = [k.sb("kn%d" % i, [128, 8, 128], BF16) for i in range(2)]
        krg = [k.sb("krg%d" % i, [128, 4, 16], F32) for i in range(2)]
        krr = [k.sb("krr%d" % i, [128, 4, 16], F32) for i in range(2)]
        kr8 = [k.sb("kr8%d" % i, [128, 8, 64], BF16) for i in range(2)]
        ktmp = [k.sb("ktmp%d" % i, [128, 2, 16], F32) for i in range(4)]
        kTn_st = [k.sb("kTn_st%d" % i, [128, 8, 512], BF16) for i in range(2)]
        kTr_st = [k.sb("kTr_st%d" % i, [64, 8, 512], BF16) for i in range(2)]
        v_st = [k.sb("v_st%d" % i, [128, 8, 4, 129], BF16) for i in range(2)]
        for vs_ in v_st:
            k.C('dve', [], [vs_], lambda e, vs_=vs_: e.memset(vs_[:, :, :, 128:129], 1.0))
        for kt_i in range(NKT):
            isc = kt_i >= L // 128
            i = kt_i % 2
            g4, j4 = kt_i // 4, kt_i % 4
            lti = lt[kt_i % 3]
            if isc:
                r0 = (kt_i - L // 128) * 128
                k.Dm('sp', lti[:], lat_ctx[r0:r0 + 128, :], [lat_ctx], [lti], lti)
            else:
                T_ = kt_i * 128
                la = lat_all[(T_ % NT) // LCH]
                rr = (T_ // NT) * LCH + (T_ % LCH)
                k.Dm('sp', lti[:], la[rr:rr + 128, :], [la], [lti], lti)
            st = stk[i]
            k.C('act', [lti], [junk, st], lambda e: e.activation(
                out=junk[:], in_=lti[:, 0:256], func=AF.Square, accum_out=st[:, 0:1]))
            rstd_from_ssq(st, 128, 0, 1, 256)
            z = zn[i]
            k.C('dve', [lti, st, kvlg_bc], [z], lambda e: e.scalar_tensor_tensor(
                out=z[:], in0=lti[:, 0:256], scalar=st[:, 1:2], in1=kvlg_bc[:], op0=ALU.mult, op1=ALU.mult))
            for j in range(2):
                k.C('pe', [z, ident], [banks[0]], lambda e, j=j: e.transpose(
                    banks[0].bf[:, j * 128:(j + 1) * 128], z[:, j * 128:(j + 1) * 128], ident[:]), inc=(j == 1))
            zT = znT[i]
            k.C('act', [banks[0]], [zT], lambda e: e.copy(
                out=zT[:], in_=banks[0].bf[:, 0:256].rearrange("p (a b) -> p a b", b=128)))
            for n4 in range(4):
                bk = banks[1 + n4]
                for kt in range(2):
                    k.C('pe', [zT, wkv], [bk], lambda e, kt=kt, bk=bk, n4=n4: e.matmul(
                        bk[:, :], lhsT=zT[:, kt, :], rhs=wkv[:, kt, n4 * 512:(n4 + 1) * 512],
                        start=(kt == 0), stop=(kt == 1)), inc=(kt == 1))
            for h in range(NH):
                bk = banks[1 + h // 2]
                c0 = (h % 2) * 256
                k.C('act', [bk], [junk, st], lambda e, h=h, bk=bk, c0=c0: e.activation(
                    out=junk[:, 0:128], in_=bk[:, c0:c0 + 128], func=AF.Square, accum_out=st[:, 8 + h:9 + h]))
            k.C('act', [lti], [junk, st], lambda e: e.activation(
                out=junk[:, 0:64], in_=lti[:, 256:320], func=AF.Square, accum_out=st[:, 2:3]))
            k.C('dve', [st], [st], lambda e: e.tensor_scalar(
                out=st[:, 8:16], in0=st[:, 8:16], scalar1=st[:, 2:3], scalar2=None, op0=ALU.add))
            rstd_from_ssq(st, 128, 8, 16, 192, ncols=8)
            vs = v_st[g4 % 2]
            for n4 in range(4):
                bk = banks[1 + n4]
                k.C('act', [bk], [vs], lambda e, bk=bk, n4=n4: e.copy(
                    out=vs[:, 2 * n4:2 * n4 + 2, j4, 0:128],
                    in_=bk[:, :].rearrange("p (h c) -> p h c", c=256)[:, :, 128:256]))
            kni = kn[i]
            for h in range(NH):
                bk = banks[1 + h // 2]
                c0 = (h % 2) * 256
                k.C('dve', [bk, st, kng_bc], [kni], lambda e, h=h, bk=bk, c0=c0: e.scalar_tensor_tensor(
                    out=kni[:, h, :], in0=bk[:, c0:c0 + 128], scalar=st[:, 16 + h:17 + h], in1=kng_bc[:, 0:128],
                    op0=ALU.mult, op1=ALU.mult))
            kg, kr_ = krg[i], krr[i]
            k.C('pool', [lti, kng_bc], [kg], lambda e: e.tensor_tensor(
                out=kg[:].rearrange("p a b -> p (a b)"), in0=lti[:, 256:320], in1=kng_bc[:, 128:192], op=ALU.mult))
            if not isc:
                rkb = rk[kt_i % 3]
                k.Dm('sp', rkb[:], rope_k[kt_i * 128:(kt_i + 1) * 128, :], [rope_k], [rkb], rkb)
                cosv = rkb[:, 0:32].rearrange("p (a f) -> p a f", f=16)
                sinv = rkb[:, 32:64].rearrange("p (a f) -> p a f", f=16)
                x0v = kg[:, 0:4:2, :]
                x1v = kg[:, 1:4:2, :]
                A_, B_, C_, D_ = ktmp
                k.C('pool', [kg, rkb], [A_], lambda e: e.tensor_tensor(out=A_[:], in0=x0v, in1=cosv, op=ALU.mult))
                k.C('pool', [kg, rkb], [B_], lambda e: e.tensor_tensor(out=B_[:], in0=x1v, in1=sinv, op=ALU.mult))
                k.C('pool', [kg, rkb], [C_], lambda e: e.tensor_tensor(out=C_[:], in0=x1v, in1=cosv, op=ALU.mult))
                k.C('pool', [kg, rkb], [D_], lambda e: e.tensor_tensor(out=D_[:], in0=x0v, in1=sinv, op=ALU.mult))
                k.C('pool', [A_, B_], [kr_], lambda e: e.tensor_tensor(out=kr_[:, 0:4:2, :], in0=A_[:], in1=B_[:], op=ALU.subtract))
                k.C('pool', [C_, D_], [kr_], lambda e: e.tensor_tensor(out=kr_[:, 1:4:2, :], in0=C_[:], in1=D_[:], op=ALU.add))
                ksrc = kr_
            else:
                ksrc = kg
            k8 = kr8[i]
            for h in range(NH):
                k.C('pool', [ksrc, st], [k8], lambda e, h=h, ksrc=ksrc: e.tensor_scalar(
                    out=k8[:, h, :], in0=ksrc[:].rearrange("p a b -> p (a b)"), scalar1=st[:, 16 + h:17 + h],
                    scalar2=None, op0=ALU.mult))
            for h in range(NH):
                k.C('pe', [kni, ident], [banks[5]], lambda e, h=h: e.transpose(
                    banks[5].bf[:, h * 128:(h + 1) * 128], kni[:, h, :], ident[:]), inc=(h == 7))
            for h in range(NH):
                k.C('pe', [k8, ident], [banks[6]], lambda e, h=h: e.transpose(
                    banks[6].bf[0:64, h * 128:(h + 1) * 128], k8[:, h, :], ident[:]), inc=(h == 7))
            tn, tr = kTn_st[g4 % 2], kTr_st[g4 % 2]
            k.C('act', [banks[5]], [tn], lambda e: e.copy(
                out=tn[:, :, j4 * 128:(j4 + 1) * 128], in_=banks[5].bf[:, 0:1024].rearrange("p (h t) -> p h t", t=128)))
            k.C('dve', [banks[6]], [tr], lambda e: e.tensor_copy(
                out=tr[:, :, j4 * 128:(j4 + 1) * 128], in_=banks[6].bf[0:64, 0:1024].rearrange("p (h t) -> p h t", t=128)))
            last = (j4 == 3) or (kt_i == NKT - 1)
            if last:
                nn = (j4 + 1) * 128
                k0 = g4 * 512
                k.Dm('sp', kT_d.t.ap()[:, 0:128, k0:k0 + nn].rearrange("h d t -> d h t"), tn[:, :, 0:nn], [tn], [kT_d], tn)
                k.Dm('sp', kT_d.t.ap()[:, 128:192, k0:k0 + nn].rearrange("h d t -> d h t"), tr[:, :, 0:nn], [tr], [kT_d], tr)
                k.Dm('sp', vv_d.t.ap()[:, :, g4 * 4:g4 * 4 + j4 + 1, :].rearrange("h p t d -> p h t d"),
                     vs[:, :, 0:j4 + 1, :], [vs], [vv_d], vs)
        k.end_phase()

        k.phase()
        kTn = k.sb("kTn", [128, NK], BF16)
        kTr = k.sb("kTr", [64, NK], BF16)
        vh = k.sb("vh", [128, NKT, 129], BF16)
        qn_ = k.sb("qn_", [128, NQ], BF16)
        qr_ = k.sb("qr_", [64, NQ], BF16)
        pT = [k.sb("pT%d" % i, [128, 512], BF16) for i in range(4)]
        osb = [k.sb("osb%d" % i, [128, 128], BF16) for i in range(2)]
        rinv = [k.sb("rinv%d" % i, [128, 1], F32) for i in range(2)]
        aT = [k.sb("aT%d" % i, [128, 512], BF16) for i in range(2)]
        if l + 1 < DEPTH:
            gather_weights(l + 1)
        qblocks = [(qb * 512, 512, 0, NKT) for qb in range(NT // 512)]
        if ctx_out:
            qblocks.append((NT, NCX, L // 128, NKT))
        for h in range(NH):
            k.Dm('sp', kTn[:], kT_d[h, 0:128, :], [kT_d], [kTn], kTn)
            k.Dm('sp', kTr[:], kT_d[h, 128:192, :], [kT_d], [kTr], kTr)
            k.Dm('sp', vh[:], vv_d[h, :, :, :], [vv_d], [vh], vh)
            k.Dm('sp', qn_[:], qT_d[h, 0:128, :], [qT_d], [qn_], qn_)
            k.Dm('sp', qr_[:], qT_d[h, 128:192, :], [qT_d], [qr_], qr_)
            for bi, (q0, nq, kt0, kt1) in enumerate(qblocks):
                nsub = nq // 128

                def emit_s(kt):
                    bk = banks[kt % 2]
                    k.C('pe', [kTn, qn_], [bk], lambda e: e.matmul(
                        bk[:, 0:nq], lhsT=kTn[:, kt * 128:(kt + 1) * 128], rhs=qn_[:, q0:q0 + nq],
                        start=True, stop=False), inc=False)
                    k.C('pe', [kTr, qr_], [bk], lambda e: e.matmul(
                        bk[:, 0:nq], lhsT=kTr[:, kt * 128:(kt + 1) * 128], rhs=qr_[:, q0:q0 + nq],
                        start=False, stop=True), inc=True)

                emit_s(kt0)
                for kt in range(kt0, kt1):
                    bk = banks[kt % 2]
                    p = pT[kt % 4]
                    k.C('act', [bk], [p], lambda e, bk=bk, p=p: e.activation(
                        out=p[:, 0:nq], in_=bk[:, 0:nq], func=AF.Exp, scale=SCALE))
                    if kt + 1 < kt1:
                        emit_s(kt + 1)
                    for s in range(nsub):
                        ob = banks[2 + s]
                        k.C('pe', [p, vh], [ob], lambda e, s=s, ob=ob, p=p, kt=kt: e.matmul(
                            ob[:, 0:129], lhsT=p[:, s * 128:(s + 1) * 128], rhs=vh[:, kt, :],
                            start=(kt == kt0), stop=(kt == kt1 - 1)), inc=(kt == kt1 - 1 or s == nsub - 1))
                a = aT[bi % 2]
                for s in range(nsub):
                    ob = banks[2 + s]
                    ri, os_ = rinv[s % 2], osb[s % 2]
                    k.C('dve', [ob], [ri], lambda e, ob=ob, ri=ri: e.reciprocal(out=ri[:], in_=ob[:, 128:129]))
                    k.C('dve', [ob, ri], [os_], lambda e, ob=ob, ri=ri, os_=os_: e.tensor_scalar(
                        out=os_[:], in0=ob[:, 0:128], scalar1=ri[:, 0:1], scalar2=None, op0=ALU.mult))
                    k.C('pe', [os_, ident], [banks[6]], lambda e, s=s, os_=os_: e.transpose(
                        banks[6].bf[:, s * 128:(s + 1) * 128], os_[:], ident[:]))
                k.C('act', [banks[6]], [a], lambda e, a=a: e.copy(out=a[:, 0:nq], in_=banks[6].bf[:, 0:nq]))
                k.Dm('sp', attnT_d[h * 128:(h + 1) * 128, q0:q0 + nq], a[:, 0:nq], [a], [attnT_d], a)
        k.end_phase()

        k.phase()
        n1 = load_vecT("n1b", n1T[l, :, :], 16)
        geL, shL = mod_vecs(l, 0, n1, 0, "L")
        geC, shC = mod_vecs(l, 0, n1, 1, "C")
        psc = load_vecT("psc", pscT[l, :, :], 4)
        sgng_bc = load_bc("sgng_bc", sgng[l:l + 1, :], 512)
        mb2 = load_bc("mb2", m_b2.t.ap().rearrange("(o a) b -> o (a b)", o=1), 144)
        g1bc = k.sb("g1bc", [128, D], F32)
        bctmp = k.sb("bctmp", [128, 128], F32)
        sgbr = k.sb("sgbr", [1, 512], BF16)
        sgbf = k.sb("sgbf", [1, 512], F32)
        k.Dm('sp', sgbf[:], sgb[l:l + 1, :], [sgb], [sgbf], sgbf)
        k.C('dve', [sgbf], [sgbr], lambda e: e.tensor_copy(out=sgbr[:], in_=sgbf[:]))
        poolw = k.sb("poolw", [128, 4, 128], BF16)
        k.Dm('sp', poolw[:], Wb["pool_w"][l].t.ap().rearrange("(g c) d -> c g d", c=128), [Wb["pool_w"][l]], [poolw], poolw)
        sgw = k.sb("sgw", [128, 4, 128], BF16)
        k.Dm('sp', sgw[:], Wb["sg_w"][l].t.ap().rearrange("(g p) q -> p g q", p=128), [Wb["sg_w"][l]], [sgw], sgw)
        sgwT = k.sb("sgwT", [128, 4, 128], BF16)
        for g in range(4):
            k.C('pe', [sgw, ident], [banks[0]], lambda e, g=g: e.transpose(
                banks[0].bf[:, g * 128:(g + 1) * 128], sgw[:, g, :], ident[:]), inc=(g == 3))
        k.C('act', [banks[0]], [sgwT], lambda e: e.copy(
            out=sgwT[:], in_=banks[0].bf[:, 0:512].rearrange("p (g t) -> p g t", t=128)))
        nb = norm_setup("B", 1)
        hT = k.sb("hTb", [128, 16, 528], BF16)
        NWCH = 4
        wch = [k.sb("wch%d" % i, [128, 16, 512], BF16) for i in range(NWCH)]
        wci = [0]

        def wchunk(wbuf, row_kt, c0, ncol=512, kt0=0):
            b = wch[wci[0] % NWCH]
            wci[0] += 1
            v = wbuf.t.ap().rearrange("(kt p) c -> p kt c", p=128)
            k.Dm('sp', b[:, 0:row_kt, 0:ncol], v[:, kt0:kt0 + row_kt, c0:c0 + ncol], [wbuf], [b], b)
            return b

        zp = [k.sb("zp%d" % g, [128, 528], F32) for g in range(4)]
        s_a = k.sb("s_a", [128, 528], F32)
        s_b = k.sb("s_b", [128, 528], F32)
        icn = k.sb("icn", [128, 4, 512], F32)
        pooled = k.sb("pooled", [128, 4, 512], BF16)
        prev_isc = [None]
        poT = k.sb("poT", [128, 4, 512], BF16)
        uT = k.sb("uT", [128, 4, 512], BF16)
        soT = k.sb("soT", [128, 4, 512], BF16)
        vg = [k.sb("vg%d" % i, [128, 512], F32) for i in range(2)]
        vn = [k.sb("vn%d" % i, [128, 512], BF16) for i in range(2)]
        stv = [k.sb("stv%d" % i, [128, 8], F32) for i in range(2)]
        atT = k.sb("atT", [128, 8, 512], BF16)
        gts = [k.sb("gts%d" % i, [128, 512], BF16) for i in range(6)]
        yT = k.sb("yT", [128, 16, 512], BF16)
        t1 = [k.sb("t1_%d" % i, [128, 512], F32) for i in range(3)]
        xsl = [k.sb("xsl%d" % i, [128, 512], F32) for i in range(2)]
        osl = [k.sb("osl%d" % i, [128, 512], F32) for i in range(2)]
        rot = [0]

        def gbank():
            b = banks[rot[0] % 6]
            rot[0] += 1
            return b

        WIN = Wb["w_in"][l]
        blocks = [(XE, i * 512, 512, False, i) for i in range(NT // 512)]
        if ctx_out:
            blocks.append((CE, 0, NCX, True, 8))
        for (src, t0, nmain, isc, bidx) in blocks:
            ge, sh = (geC, shC) if isc else (geL, shL)
            if prev_isc[0] != isc:
                make_bc(l, 32, 1 if isc else 0, g1bc, bctmp)
                prev_isc[0] = isc
            g1 = g1bc
            dst = cmid if isc else xmid
            ntile = nmain // 128
            norm_tile(nb, src, src[t0:t0 + 8, :], 8, ge, sh, hT, 0, (banks[6], banks[7]))
            for s in range(ntile):
                norm_tile(nb, src, src[t0 + 8 + s * 128:t0 + 8 + (s + 1) * 128, :], 128, ge, sh, hT, 8 + s * 128,
                          (banks[6], banks[7]))
            norm_tile(nb, src, src[t0 + 8 + nmain:t0 + 16 + nmain, :], 8, ge, sh, hT, 8 + nmain, (banks[6], banks[7]))
            hr = 8 + nmain
            wp = wchunk(WIN, 16, 0)
            for g in range(4):
                bk = gbank()
                for kt in range(16):
                    k.C('pe', [hT, wp], [bk], lambda e, kt=kt, g=g, bk=bk: e.matmul(
                        bk[:, 0:nmain], lhsT=wp[:, kt, g * 128:(g + 1) * 128], rhs=hT[:, kt, 8:8 + nmain],
                        start=(kt == 0), stop=(kt == 15)), inc=(kt == 15))
                k.C('act', [bk], [zp[g]], lambda e, g=g, bk=bk: e.copy(out=zp[g][:, 8:8 + nmain], in_=bk[:, 0:nmain]))
                bk2 = gbank()
                for kt in range(16):
                    k.C('pe', [hT, wp], [bk2], lambda e, kt=kt, g=g, bk2=bk2: e.matmul(
                        bk2[:, 0:8], lhsT=wp[:, kt, g * 128:(g + 1) * 128], rhs=hT[:, kt, 0:8],
                        start=(kt == 0), stop=(kt == 15)), inc=(kt == 15))
                k.C('dve', [bk2, mb2], [zp[g]], lambda e, g=g, bk2=bk2: e.tensor_tensor(
                    out=zp[g][:, 0:8], in0=bk2[:, 0:8], in1=mb2[:, bidx * 16:bidx * 16 + 8], op=ALU.mult))
                bk3 = gbank()
                for kt in range(16):
                    k.C('pe', [hT, wp], [bk3], lambda e, kt=kt, g=g, bk3=bk3: e.matmul(
                        bk3[:, 0:8], lhsT=wp[:, kt, g * 128:(g + 1) * 128], rhs=hT[:, kt, hr:hr + 8],
                        start=(kt == 0), stop=(kt == 15)), inc=(kt == 15))
                k.C('dve', [bk3, mb2], [zp[g]], lambda e, g=g, bk3=bk3: e.tensor_tensor(
                    out=zp[g][:, hr:hr + 8], in0=bk3[:, 0:8], in1=mb2[:, bidx * 16 + 8:bidx * 16 + 16], op=ALU.mult))
            ic0 = (NT + 0) if isc else t0
            k.Dm('sp', icn[:, :, 0:nmain], invc.t.ap()[:, ic0:ic0 + nmain].rearrange("(o g) t -> o g t", o=1).to_broadcast([128, 4, nmain]),
                 [invc], [icn], icn)
            Wd = 16 + nmain
            for g in range(4):
                z = zp[g]
                k.C('pool', [z], [s_a], lambda e, z=z: e.tensor_tensor(out=s_a[:, 1:Wd], in0=z[:, 0:Wd - 1], in1=z[:, 1:Wd], op=ALU.add))
                cur, oth = s_a, s_b
                lo, hi = 1, Wd
                half = 1
                for lev in range(g):
                    nlo, nhi = lo + half, hi - half
                    k.C('pool', [cur], [oth], lambda e, cur=cur, oth=oth, nlo=nlo, nhi=nhi, half=half: e.tensor_tensor(
                        out=oth[:, nlo:nhi], in0=cur[:, nlo - half:nhi - half], in1=cur[:, nlo + half:nhi + half], op=ALU.add))
                    cur, oth = oth, cur
                    lo, hi = nlo, nhi
                    half *= 2
                k.C('dve', [cur, icn], [oth], lambda e, cur=cur, oth=oth, g=g: e.tensor_tensor(
                    out=oth[:, 8:8 + nmain], in0=cur[:, 8:8 + nmain], in1=icn[:, g, 0:nmain], op=ALU.mult))
                k.C('dve', [oth, z], [pooled], lambda e, oth=oth, z=z, g=g: e.tensor_tensor(
                    out=pooled[:, g, 0:nmain], in0=oth[:, 8:8 + nmain], in1=z[:, 8:8 + nmain], op=ALU.subtract))
                bk = gbank()
                k.C('pe', [pooled, poolw], [bk], lambda e, g=g, bk=bk: e.matmul(
                    bk[:, 0:nmain], lhsT=poolw[:, g, :], rhs=pooled[:, g, 0:nmain], start=True, stop=True))
                k.C('act', [bk, psc], [poT], lambda e, g=g, bk=bk: e.activation(
                    out=poT[:, g, 0:nmain], in_=bk[:, 0:nmain], func=AF.Identity, scale=psc[:, g:g + 1]))
            wu = wchunk(WIN, 16, 1344)
            for g in range(4):
                bk = gbank()
                for kt in range(16):
                    k.C('pe', [hT, wu], [bk], lambda e, kt=kt, g=g, bk=bk: e.matmul(
                        bk[:, 0:nmain], lhsT=wu[:, kt, g * 128:(g + 1) * 128], rhs=hT[:, kt, 8:8 + nmain],
                        start=(kt == 0), stop=(kt == 15)), inc=(kt == 15))
                k.C('act', [bk], [uT], lambda e, g=g, bk=bk: e.activation(
                    out=uT[:, g, 0:nmain], in_=bk[:, 0:nmain], func=AF.Gelu_apprx_tanh))
            wv = wchunk(WIN, 16, 1856)
            sbk = [banks[0], banks[1], banks[2], banks[3]]
            for s in range(ntile):
                bk = banks[4 + s % 2]
                for kt in range(16):
                    k.C('pe', [hT, wv], [bk], lambda e, kt=kt, s=s, bk=bk: e.matmul(
                        bk[:, :], lhsT=hT[:, kt, 8 + s * 128:8 + (s + 1) * 128], rhs=wv[:, kt, 0:512],
                        start=(kt == 0), stop=(kt == 15)), inc=(kt == 15))
                vgi, vni, sti = vg[s % 2], vn[s % 2], stv[s % 2]
                k.C('act', [bk], [vgi, sti], lambda e, bk=bk, vgi=vgi, sti=sti: e.activation(
                    out=vgi[:], in_=bk[:, :], func=AF.Gelu_apprx_tanh, accum_out=sti[:, 0:1]))
                k.C('dve', [sti], [sti], lambda e, sti=sti: e.tensor_scalar(
                    out=sti[:, 1:2], in0=sti[:, 0:1], scalar1=-1.0 / 512, scalar2=None, op0=ALU.mult))
                k.C('act', [vgi, sti], [vgi], lambda e, vgi=vgi, sti=sti: e.activation(
                    out=vgi[:], in_=vgi[:], func=AF.Identity, bias=sti[:, 1:2]))
                k.C('act', [vgi], [vni, sti], lambda e, vgi=vgi, vni=vni, sti=sti: e.activation(
                    out=vni[:], in_=vgi[:], func=AF.Square, accum_out=sti[:, 2:3]))
                rstd_from_ssq(sti, 128, 2, 3, 512)
                k.C('dve', [vgi, sti, sgng_bc], [vni], lambda e, vgi=vgi, vni=vni, sti=sti: e.scalar_tensor_tensor(
                    out=vni[:], in0=vgi[:], scalar=sti[:, 3:4], in1=sgng_bc[:], op0=ALU.mult, op1=ALU.mult))
                for g in range(4):
                    k.C('pe', [vni, sgwT], [sbk[g]], lambda e, g=g, s=s, vni=vni: e.matmul(
                        sbk[g][:, s * 128:(s + 1) * 128], lhsT=vni[:, g * 128:(g + 1) * 128], rhs=sgwT[:, g, :],
                        start=True, stop=False), inc=False)
                    k.C('pe', [ones1, sgbr], [sbk[g]], lambda e, g=g, s=s: e.matmul(
                        sbk[g][:, s * 128:(s + 1) * 128], lhsT=ones1[0:1, :], rhs=sgbr[0:1, g * 128:(g + 1) * 128],
                        start=False, stop=True), inc=True)
            for g in range(4):
                k.C('dve', [sbk[g], uT], [soT], lambda e, g=g: e.tensor_tensor(
                    out=soT[:, g, 0:nmain], in0=sbk[g][:, 0:nmain], in1=uT[:, g, 0:nmain], op=ALU.mult))
            q0 = NT if isc else t0
            k.Dm('sp', atT[:, :, 0:nmain], attnT_d.t.ap().rearrange("(kt p) t -> p kt t", p=128)[:, :, q0:q0 + nmain],
                 [attnT_d], [atT], atT)
            gi = 0
            for dg in range(4):
                gch = [wchunk(WIN, 16, 2368 + br * 2048 + dg * 512) for br in range(3)]
                wbr = wch[wci[0] % NWCH]
                wci[0] += 1
                for (nm, kt0, nk) in (("w_br_pool", 0, 4), ("w_br_mla", 4, 8), ("w_br_sg", 12, 4)):
                    v = Wb[nm][l].t.ap().rearrange("(kt p) c -> p kt c", p=128)
                    k.Dm('sp', wbr[:, kt0:kt0 + nk, :], v[:, :, dg * 512:(dg + 1) * 512], [Wb[nm][l]], [wbr], wbr)
                for jj in range(4):
                    dt_ = dg * 4 + jj
                    gt = []
                    for br in range(3):
                        bk = gbank()
                        gc = gch[br]
                        for kt in range(16):
                            k.C('pe', [hT, gc], [bk], lambda e, kt=kt, jj=jj, bk=bk, gc=gc: e.matmul(
                                bk[:, 0:nmain], lhsT=gc[:, kt, jj * 128:(jj + 1) * 128], rhs=hT[:, kt, 8:8 + nmain],
                                start=(kt == 0), stop=(kt == 15)), inc=(kt == 15))
                        gb = gts[gi % 6]
                        gi += 1
                        k.C('act', [bk], [gb], lambda e, bk=bk, gb=gb: e.activation(
                            out=gb[:, 0:nmain], in_=bk[:, 0:nmain], func=AF.Sigmoid))
                        gt.append(gb)
                    tt = []
                    for br, (kt0, nk, srcT) in enumerate(((0, 4, poT), (4, 8, atT), (12, 4, soT))):
                        bk = gbank()
                        for kt in range(nk):
                            k.C('pe', [srcT, wbr], [bk], lambda e, kt=kt, kt0=kt0, nk=nk, jj=jj, bk=bk, srcT=srcT: e.matmul(
                                bk[:, 0:nmain], lhsT=wbr[:, kt0 + kt, jj * 128:(jj + 1) * 128], rhs=srcT[:, kt, 0:nmain],
                                start=(kt == 0), stop=(kt == nk - 1)), inc=(kt == nk - 1))
                        tb = t1[br]
                        k.C('dve', [bk, gt[br]], [tb], lambda e, bk=bk, br=br, tb=tb, gt=gt: e.tensor_tensor(
                            out=tb[:, 0:nmain], in0=bk[:, 0:nmain], in1=gt[br][:, 0:nmain], op=ALU.mult))
                        tt.append(tb)
                    k.C('pool', [tt[0], tt[1]], [tt[0]], lambda e, tt=tt: e.tensor_tensor(
                        out=tt[0][:, 0:nmain], in0=tt[0][:, 0:nmain], in1=tt[1][:, 0:nmain], op=ALU.add))
                    k.C('pool', [tt[0], tt[2]], [yT], lambda e, tt=tt, dt_=dt_: e.tensor_tensor(
                        out=yT[:, dt_, 0:nmain], in0=tt[0][:, 0:nmain], in1=tt[2][:, 0:nmain], op=ALU.add))
            xi = 0
            for c in range(4):
                wo = wchunk(Wb["w_o"][l], 16, c * 512)
                for s in range(ntile):
                    bk = gbank()
                    for kt in range(16):
                        k.C('pe', [yT, wo], [bk], lambda e, kt=kt, s=s, bk=bk, wo=wo: e.matmul(
                            bk[:, :], lhsT=yT[:, kt, s * 128:(s + 1) * 128], rhs=wo[:, kt, 0:512],
                            start=(kt == 0), stop=(kt == 15)), inc=(kt == 15))
                    xs_, os_ = xsl[xi % 2], osl[xi % 2]
                    tb = t1[xi % 3]
                    xi += 1
                    r0 = t0 + 8 + s * 128
                    k.Dm('sp', xs_[:], src[r0:r0 + 128, c * 512:(c + 1) * 512], [src], [xs_], xs_)
                    k.C('dve', [bk, g1], [tb], lambda e, bk=bk, tb=tb, c=c: e.tensor_tensor(
                        out=tb[:], in0=bk[:, :], in1=g1[:, c * 512:(c + 1) * 512], op=ALU.mult))
                    k.C('pool', [tb, xs_], [os_], lambda e, tb=tb, xs_=xs_, os_=os_: e.tensor_tensor(
                        out=os_[:], in0=tb[:], in1=xs_[:], op=ALU.add))
                    d0 = 1 + t0 + s * 128
                    k.Dm('sp', dst[d0:d0 + 128, c * 512:(c + 1) * 512], os_[:], [os_], [dst], os_)
        k.end_phase()
        k.phase()
        exs = k.sb("exs", [8, D], F32)
        exo = k.sb("exo", [2, D], F32)
        s2 = k.sb("s2", [8, 2], F32)
        k.Dm('sp', ex2_in[0:1, :], xmid[1:2, :], [xmid], [ex2_in], exs)
        k.Dm('sp', ex2_in[1:2, :], xmid[NT:NT + 1, :], [xmid], [ex2_in], exs)
        k.Dm('sp', s2[:], sel2[:, :], [sel2], [s2], s2)
        k.barrier()
        all_gather(ex2_in, ex2_out)
        k.Dm('sp', exs[:], ex2_out[:, :], [ex2_out], [exs], exs)
        for c in range(4):
            k.C('pe', [exs, s2], [banks[c]], lambda e, c=c: e.matmul(
                banks[c][0:2, :], lhsT=s2[:], rhs=exs[:, c * 512:(c + 1) * 512], start=True, stop=True))
            k.C('act', [banks[c]], [exo], lambda e, c=c: e.copy(out=exo[:, c * 512:(c + 1) * 512], in_=banks[c][0:2, :]))
        k.Dm('sp', xmid[0:1, :], exo[0:1, :], [exo], [xmid], exo)
        k.Dm('sp', xmid[NT + 1:NT + 2, :], exo[1:2, :], [exo], [xmid], exo)
        k.end_phase()

        k.phase()
        n2 = load_vecT("n2", n2T[l, :, :], 16)
        geL, shL = mod_vecs(l, 1, n2, 0, "L")
        geC, shC = mod_vecs(l, 1, n2, 1, "C")
        g2bc = k.sb("g2bc", [128, D], F32)
        bctmp = k.sb("bctmp", [128, 128], F32)
        cw = load_vecT("cw", cwT[l, :, :], 264)
        cb = load_vecT("cb", cbT[l, :, :], 88)
        mcb = load_bc("mcb", m_c.t.ap().rearrange("(o a) b -> o (a b)", o=1), 20)
        nb = norm_setup("C", 1)
        hT = k.sb("h2T", [128, 16, 512], BF16)
        actT = k.sb("actT", [128, 44, 512], BF16)
        NWCH = 4
        wch = [k.sb("wch%d" % i, [128, 16, 512], BF16) for i in range(NWCH)]
        wci = [0]

        def wchunk(wbuf, row_kt, c0, ncol=512, kt0=0):
            b = wch[wci[0] % NWCH]
            wci[0] += 1
            v = wbuf.t.ap().rearrange("(kt p) c -> p kt c", p=128)
            k.Dm('sp', b[:, 0:row_kt, 0:ncol], v[:, kt0:kt0 + row_kt, c0:c0 + ncol], [wbuf], [b], b)
            return b

        ta = [k.sb("ta%d" % i, [128, 512], F32) for i in range(2)]
        tv = [k.sb("tv%d" % i, [128, 512], F32) for i in range(2)]
        sa = [k.sb("sa%d" % i, [128, 512], F32) for i in range(2)]
        t1 = [k.sb("t1c%d" % i, [128, 512], F32) for i in range(2)]
        xsl = [k.sb("xslc%d" % i, [128, 512], F32) for i in range(2)]
        osl = [k.sb("oslc%d" % i, [128, 512], F32) for i in range(2)]
        rot = [0]

        def gbank():
            b = banks[rot[0] % 6]
            rot[0] += 1
            return b

        FU, FD = Wb["ffn_up"][l], Wb["ffn_down"][l]
        blocks = []
        nblk = (NT + 509) // 510
        for i in range(nblk):
            p0 = 510 * i
            blocks.append((xmid, p0, min(510, NT - p0), False, i))
        if ctx_out:
            blocks.append((cmid, 0, NCX, True, 9))
        prev_isc = [None]
        ji = 0
        for (src, p0, npay, isc, bidx) in blocks:
            ge, sh = (geC, shC) if isc else (geL, shL)
            if prev_isc[0] != isc:
                make_bc(l, 80, 1 if isc else 0, g2bc, bctmp)
                prev_isc[0] = isc
            ncols = npay + 2
            c0 = 0
            while c0 < ncols:
                n = min(128, ncols - c0)
                norm_tile(nb, src, src[p0 + c0:p0 + c0 + n, :], n, ge, sh, hT, c0, (banks[6], banks[7]))
                c0 += n
            k.C('dve', [hT, mcb], [hT], lambda e: e.tensor_scalar(
                out=hT[:, :, 0:1], in0=hT[:, :, 0:1], scalar1=mcb[:, 2 * bidx:2 * bidx + 1], scalar2=None, op0=ALU.mult))
            k.C('dve', [hT, mcb], [hT], lambda e: e.tensor_scalar(
                out=hT[:, :, ncols - 1:ncols], in0=hT[:, :, ncols - 1:ncols], scalar1=mcb[:, 2 * bidx + 1:2 * bidx + 2],
                scalar2=None, op0=ALU.mult))
            for jg in range(11):
                wa_ = wchunk(FU, 16, jg * 512)
                wv_ = wchunk(FU, 16, DFF + jg * 512)
                for jj in range(4):
                    j = jg * 4 + jj
                    res = []
                    for (wgt, cj, tbuf) in ((wa_, j, ta[ji % 2]), (wv_, 44 + j, tv[ji % 2])):
                        bk = gbank()
                        for kt in range(16):
                            k.C('pe', [hT, wgt], [bk], lambda e, kt=kt, jj=jj, bk=bk, wgt=wgt: e.matmul(
                                bk[:, 0:ncols], lhsT=wgt[:, kt, jj * 128:(jj + 1) * 128], rhs=hT[:, kt, 0:ncols],
                                start=(kt == 0), stop=(kt == 15)), inc=(kt == 15))
                        k.C('act', [bk, cw, cb], [tbuf], lambda e, bk=bk, cj=cj, tbuf=tbuf: e.activation(
                            out=tbuf[:, 0:npay], in_=bk[:, 1:1 + npay], func=AF.Identity,
                            scale=cw[:, 88 + cj:88 + cj + 1], bias=cb[:, cj:cj + 1]))
                        k.C('dve', [bk, cw, tbuf], [tbuf], lambda e, bk=bk, cj=cj, tbuf=tbuf: e.scalar_tensor_tensor(
                            out=tbuf[:, 0:npay], in0=bk[:, 0:npay], scalar=cw[:, cj:cj + 1], in1=tbuf[:, 0:npay],
                            op0=ALU.mult, op1=ALU.add))
                        k.C('dve', [bk, cw, tbuf], [tbuf], lambda e, bk=bk, cj=cj, tbuf=tbuf: e.scalar_tensor_tensor(
                            out=tbuf[:, 0:npay], in0=bk[:, 2:2 + npay], scalar=cw[:, 176 + cj:176 + cj + 1],
                            in1=tbuf[:, 0:npay], op0=ALU.mult, op1=ALU.add))
                        res.append(tbuf)
                    sb_ = sa[ji % 2]
                    ji += 1
                    k.C('act', [res[0]], [sb_], lambda e, res=res, sb_=sb_: e.activation(
                        out=sb_[:, 0:npay], in_=res[0][:, 0:npay], func=AF.Silu))
                    k.C('pool', [sb_, res[1]], [actT], lambda e, res=res, sb_=sb_, j=j: e.tensor_tensor(
                        out=actT[:, j, 0:npay], in0=sb_[:, 0:npay], in1=res[1][:, 0:npay], op=ALU.mult))
            subs = []
            s0 = 0
            while s0 < npay:
                m = min(128, npay - s0)
                subs.append((s0, m))
                s0 += m
            if isc:
                dstb, doff = cout_e[l], 8
            elif l == DEPTH - 1:
                dstb, doff = out_d, 0
            else:
                dstb, doff = xout_e[l], 8
            xi = 0
            for c in range(4):
                for part in range(4):
                    wd = wchunk(FD, 11, c * 512, kt0=part * 11)
                    for si, (s0, m) in enumerate(subs):
                        bk = banks[si]
                        for kk in range(11):
                            kt = part * 11 + kk
                            k.C('pe', [actT, wd], [bk], lambda e, kt=kt, kk=kk, s0=s0, m=m, bk=bk, wd=wd: e.matmul(
                                bk[0:m, :], lhsT=actT[:, kt, s0:s0 + m], rhs=wd[:, kk, 0:512],
                                start=(kt == 0), stop=(kt == 43)), inc=(kk == 10))
                for si, (s0, m) in enumerate(subs):
                    bk = banks[si]
                    xs_, os_, tb = xsl[xi % 2], osl[xi % 2], t1[xi % 2]
                    xi += 1
                    r0 = 1 + p0 + s0
                    k.Dm('sp', xs_[0:m, :], src[r0:r0 + m, c * 512:(c + 1) * 512], [src], [xs_], xs_)
                    k.C('dve', [bk, g2bc], [tb], lambda e, bk=bk, tb=tb, c=c, m=m: e.tensor_tensor(
                        out=tb[0:m, :], in0=bk[0:m, :], in1=g2bc[0:m, c * 512:(c + 1) * 512], op=ALU.mult))
                    k.C('pool', [tb, xs_], [os_], lambda e, tb=tb, xs_=xs_, os_=os_, m=m: e.tensor_tensor(
                        out=os_[0:m, :], in0=tb[0:m, :], in1=xs_[0:m, :], op=ALU.add))
                    d0 = doff + p0 + s0
                    k.Dm('sp', dstb[d0:d0 + m, c * 512:(c + 1) * 512], os_[0:m, :], [os_], [dstb], os_)
        if l < DEPTH - 1:
            k.end_phase()
            k.phase()
            exs3 = k.sb("exs3", [64, D], F32)
            exo3 = k.sb("exo3", [16, D], F32)
            s3 = k.sb("s3", [64, 16], F32)
            k.Dm('sp', ex3_in[0:8, :], x1e[8:16, :], [x1e], [ex3_in], exs3)
            k.Dm('sp', ex3_in[8:16, :], x1e[NT:NT + 8, :], [x1e], [ex3_in], exs3)
            k.Dm('sp', s3[:], sel3[:, :], [sel3], [s3], s3)
            k.barrier()
            all_gather(ex3_in, ex3_out)
            k.Dm('sp', exs3[:], ex3_out[:, :], [ex3_out], [exs3], exs3)
            for c in range(4):
                k.C('pe', [exs3, s3], [banks[c]], lambda e, c=c: e.matmul(
                    banks[c][0:16, :], lhsT=s3[:], rhs=exs3[:, c * 512:(c + 1) * 512], start=True, stop=True))
                k.C('act', [banks[c]], [exo3], lambda e, c=c: e.copy(out=exo3[:, c * 512:(c + 1) * 512], in_=banks[c][0:16, :]))
            k.Dm('sp', x1e[0:8, :], exo3[0:8, :], [exo3], [x1e], exo3)
            k.Dm('sp', x1e[8 + NT:16 + NT, :], exo3[8:16, :], [exo3], [x1e], exo3)
        k.end_phase()
    k.n_total = k.n_ins
    print("kernel: instructions emitted ~", k.n_ins, "milestones", k.cnt)
    return nc


def _rope_tables(n):
    rows = n // 64
    row = np.repeat(np.arange(rows, dtype=np.float32), 64)
    col = np.tile(np.arange(64, dtype=np.float32), rows)
    inv_freq = (np.float32(10000.0) ** (-np.arange(16, dtype=np.float32) / np.float32(16))).astype(np.float32)
    ang = np.stack([row[:, None] * inv_freq, col[:, None] * inv_freq], axis=1).astype(np.float32)
    return np.cos(ang).astype(np.float32), np.sin(ang).astype(np.float32)


def _invcnt(pos, seqlen):
    out = np.zeros((4, len(pos)), np.float32)
    for g, w in enumerate((2, 4, 8, 16)):
        lo = np.clip(pos - w // 2, 0, seqlen)
        hi = np.clip(pos + w // 2, 0, seqlen)
        out[g] = 1.0 / (hi - lo).astype(np.float32)
    return out


_NC_CACHE = {}


def kernel(**inputs):
    f32 = np.float32
    inp = {n: np.ascontiguousarray(np.asarray(v, dtype=f32)) for n, v in inputs.items()}
    x, c, ctx, c_ctx = inp["x"], inp["c"], inp["ctx"], inp["c_ctx"]
    cos, sin = _rope_tables(L)
    rope_k = np.ascontiguousarray(np.concatenate([cos.reshape(L, 32), sin.reshape(L, 32)], axis=1))
    shared = {}
    big = {}
    for n in ("w_in", "w_uq", "w_ukv", "w_br_pool", "w_br_mla", "w_br_sg", "w_o", "ffn_up", "ffn_down", "ada_w"):
        big[n] = inp[n]
    big["pool_w"] = inp["pool_w"].reshape(DEPTH, 512, 128)
    big["sg_w"] = inp["sg_w"].reshape(DEPTH, 512, 128)
    shared["adabT"] = np.ascontiguousarray(inp["ada_b"].reshape(DEPTH, 96, 128).transpose(0, 2, 1))
    shared["n1T"] = np.ascontiguousarray(inp["norm1_g"].reshape(DEPTH, 16, 128).transpose(0, 2, 1))
    shared["n2T"] = np.ascontiguousarray(inp["norm2_g"].reshape(DEPTH, 16, 128).transpose(0, 2, 1))
    shared["pscT"] = np.ascontiguousarray(inp["pool_scale"].reshape(DEPTH, 4, 128).transpose(0, 2, 1))
    shared["cwT"] = np.ascontiguousarray(
        inp["ffn_conv_w"].reshape(DEPTH, 3, 88, 128).transpose(0, 3, 1, 2).reshape(DEPTH, 128, 264))
    shared["cbT"] = np.ascontiguousarray(inp["ffn_conv_b"].reshape(DEPTH, 88, 128).transpose(0, 2, 1))
    shared["qlg"] = inp["q_lat_g"]
    shared["kvlg"] = inp["kv_lat_g"]
    shared["qng"] = inp["q_norm_g"]
    shared["kng"] = inp["k_norm_g"]
    shared["sgng"] = inp["sg_norm_g"]
    shared["sgb"] = np.ascontiguousarray(inp["sg_b"].reshape(DEPTH, 512))
    shared["identf"] = np.eye(128, dtype=f32)
    shared["rope_k"] = rope_k
    in_maps = []
    for core in range(8):
        b, q = core // 4, core % 4
        t0 = q * NT
        m = dict(shared)
        for n, wv in big.items():
            rs = wv.shape[1] // 8
            m[n] = np.ascontiguousarray(wv[:, core * rs:(core + 1) * rs, :])
        cv3 = np.stack([c[0], c[1], c_ctx], axis=0)[:, core * 256:(core + 1) * 256]
        m["cvT"] = np.ascontiguousarray(cv3.reshape(3, 2, 128).transpose(2, 1, 0).reshape(128, 6))
        m["selb"] = np.array([[1.0, 0.0]] if b == 0 else [[0.0, 1.0]], dtype=f32)
        m["x"] = np.ascontiguousarray(x[b, t0:t0 + NT])
        xh = np.zeros((16, D), f32)
        if q > 0:
            xh[0:8] = x[b, t0 - 8:t0]
        if q < 3:
            xh[8:16] = x[b, t0 + NT:t0 + NT + 8]
        m["xh"] = xh
        m["ctx"] = np.ascontiguousarray(ctx[b])
        cq = cos[t0:t0 + NT]
        sq = sin[t0:t0 + NT]
        m["rope_q"] = np.ascontiguousarray(np.concatenate(
            [np.broadcast_to(cq[:, None], (NT, 8, 2, 16)).reshape(NT, 256),
             np.broadcast_to(sq[:, None], (NT, 8, 2, 16)).reshape(NT, 256)], axis=1))
        m["invc"] = np.ascontiguousarray(np.concatenate(
            [_invcnt(np.arange(t0, t0 + NT), L), _invcnt(np.arange(NCX), NCX)], axis=1))
        mb2 = np.ones((9, 16), f32)
        if q == 0:
            mb2[0, 0:8] = 0
        if q == 3:
            mb2[NT // 512 - 1, 8:16] = 0
        mb2[8, :] = 0
        m["m_b2"] = mb2
        mc = np.ones((10, 2), f32)
        if q == 0:
            mc[0, 0] = 0
        if q == 3:
            mc[(NT + 509) // 510 - 1, 1] = 0
        mc[9, :] = 0
        m["m_c"] = mc
        s2 = np.zeros((8, 2), f32)
        if q > 0:
            s2[2 * (q - 1) + 1, 0] = 1
        if q < 3:
            s2[2 * (q + 1), 1] = 1
        m["sel2"] = s2
        s3 = np.zeros((64, 16), f32)
        for i in range(8):
            if q > 0:
                s3[16 * (q - 1) + 8 + i, i] = 1
            if q < 3:
                s3[16 * (q + 1) + i, 8 + i] = 1
        m["sel3"] = s3
        in_maps.append(m)
    import time as _time
    _t0 = _time.time()
    key = (NT, L)
    if key not in _NC_CACHE:
        _NC_CACHE[key] = build_program()
    _t1 = _time.time()
    res = run_bass_kernel_spmd(_NC_CACHE[key], in_maps, core_ids=list(range(8)))
    print("kernel: build %.1fs run %.1fs" % (_t1 - _t0, _time.time() - _t1))
    out = np.zeros((2, L, D), f32)
    for core in range(8):
        b, q = core // 4, core % 4
        out[b, q * NT:(q + 1) * NT] = res.results[core]["out"]
    return out
```
